# Optimizing a Trainium2 kernel written in Bass

```python
import jax, jax.numpy as jnp
from jax import lax
import numpy as np

D_MODEL = 1024
BATCH = 4
SEQ = 8192
DEPTH = 2
DEC_BATCH = 2
DEC_SEQ = 16384
PAST_LEN = 128

N_META = 16
GRID_W = 64
N_MIXERS = 2
N_SSD_LAYERS = (DEPTH + 1) // 2
N_ATTN_LAYERS = DEPTH // 2
RMS_EPS = 1e-6

SSD_EXPAND = 2
D_INNER = SSD_EXPAND * D_MODEL
SSD_HEAD_DIM = 64
SSD_HEADS = D_INNER // SSD_HEAD_DIM
SSD_GROUPS = 4
SSD_STATE = 128
CONV_W = 5
CONV_DIM = D_INNER + 2 * SSD_GROUPS * SSD_STATE
SSD_IN_DIM = D_INNER + CONV_DIM + 2 * SSD_HEADS
CHUNK = 128
META_PAD = CHUNK - N_META

ATTN_HEAD_DIM = 128
N_Q_HEADS = D_MODEL // ATTN_HEAD_DIM
N_KV_HEADS = 2
KV_REP = N_Q_HEADS // N_KV_HEADS
Q_WIDTH = N_Q_HEADS * ATTN_HEAD_DIM
KV_WIDTH = N_KV_HEADS * ATTN_HEAD_DIM
QKV_DIM = Q_WIDTH + 2 * KV_WIDTH
Q_BLOCK = 128
ATTN_SCALE = ATTN_HEAD_DIM ** -0.5
ROPE_THETA = 10000.0
ROPE_AXIS_DIM = ATTN_HEAD_DIM // 2
ROPE_FREQS = ROPE_AXIS_DIM // 2

PEER_HEADS = 8
N_KEYS = 128
N_EXPERTS = N_KEYS * N_KEYS
PEER_KEY_DIM = 256
PEER_HALF = PEER_KEY_DIM // 2
PEER_TOPK = 16
PEER_BLOCK = 128

kernel_name = 'bidir_hybrid_ssd_axialgqa_peer'


def rms_norm(x, g):
    xf = x.astype(jnp.float32)
    y = xf * lax.rsqrt(jnp.mean(xf * xf, axis=-1, keepdims=True) + RMS_EPS)
    return (y * g.astype(jnp.float32)).astype(x.dtype)


def ssd_chunked(x, dt, a, Bm, Cm):
    b, T, H, P = x.shape
    G, N = SSD_GROUPS, SSD_STATE
    R = H // G
    nc = T // CHUNK
    xdt = (x * dt[..., None]).reshape(b, nc, CHUNK, G, R, P)
    adt = (dt * a).reshape(b, nc, CHUNK, G, R).transpose(0, 1, 3, 4, 2)
    cs = jnp.cumsum(adt, axis=-1)
    Bc = Bm.reshape(b, nc, CHUNK, G, N)
    Cc = Cm.reshape(b, nc, CHUNK, G, N)
    lower = jnp.tril(jnp.ones((CHUNK, CHUNK), dtype=bool))
    seg = jnp.where(lower, cs[..., :, None] - cs[..., None, :], -jnp.inf)
    cb = jnp.einsum('bclgn,bcsgn->bcgls', Cc, Bc)
    w = cb[:, :, :, None] * jnp.exp(seg)
    y_diag = jnp.einsum('bcgrls,bcsgrp->bclgrp', w, xdt)
    decay_to_end = jnp.exp(cs[..., -1:] - cs)
    states = jnp.einsum('bclgn,bcgrl,bclgrp->bcgrpn', Bc, decay_to_end, xdt)
    chunk_decay = jnp.exp(cs[..., -1])

    def carry_state(h, inp):
        st, dec = inp
        return h * dec[..., None, None] + st, h

    h0 = jnp.zeros((b, G, R, P, N), jnp.float32)
    _, prev = lax.scan(carry_state, h0, (jnp.moveaxis(states, 1, 0), jnp.moveaxis(chunk_decay, 1, 0)))
    prev = jnp.moveaxis(prev, 0, 1)
    y_off = jnp.einsum('bclgn,bcgrpn,bcgrl->bclgrp', Cc, prev, jnp.exp(cs))
    return (y_diag + y_off).reshape(b, T, H, P)


def ssd_mixer(h, w_in, conv_w, conv_b, dt_bias, a_log, d_skip, gate_norm_g, w_out):
    b, L, _ = h.shape
    f32 = jnp.float32
    proj = h @ w_in
    z = proj[..., :D_INNER]
    xbc = proj[..., D_INNER:D_INNER + CONV_DIM]
    dt_raw = proj[..., D_INNER + CONV_DIM:]
    half = (CONV_W - 1) // 2
    xbc = lax.conv_general_dilated(xbc, conv_w[:, None, :], window_strides=(1,), padding=[(half, half)],
                                   dimension_numbers=('NWC', 'WIO', 'NWC'), feature_group_count=CONV_DIM)
    xbc = jax.nn.silu(xbc + conv_b)
    xs = xbc[..., :D_INNER].reshape(b, L, SSD_HEADS, SSD_HEAD_DIM)
    Bm = xbc[..., D_INNER:D_INNER + SSD_GROUPS * SSD_STATE].reshape(b, L, SSD_GROUPS, SSD_STATE)
    Cm = xbc[..., D_INNER + SSD_GROUPS * SSD_STATE:].reshape(b, L, SSD_GROUPS, SSD_STATE)
    dt = jax.nn.softplus((dt_raw.reshape(b, L, 2, SSD_HEADS) + dt_bias).astype(f32))
    a = -jnp.exp(a_log.astype(f32))

    def front(t):
        return jnp.pad(t.astype(f32), [(0, 0), (META_PAD, 0)] + [(0, 0)] * (t.ndim - 2))

    def flip(t):
        return jnp.flip(t, axis=1)

    xp, Bp, Cp, dtp = front(xs), front(Bm), front(Cm), front(dt)
    y_fwd = ssd_chunked(xp, dtp[:, :, 0], a[0], Bp, Cp)
    y_bwd = flip(ssd_chunked(flip(xp), flip(dtp[:, :, 1]), a[1], flip(Bp), flip(Cp)))
    y = (y_fwd + y_bwd)[:, META_PAD:] + d_skip.astype(f32)[:, None] * xs.astype(f32)
    y = y.reshape(b, L, D_INNER).astype(h.dtype)
    y = rms_norm(y * jax.nn.silu(z), gate_norm_g)
    return y @ w_out


def axial_rope_tables(n_tokens):
    rows = n_tokens // GRID_W
    tok_row = jnp.repeat(jnp.arange(rows), GRID_W)
    tok_col = jnp.tile(jnp.arange(GRID_W), rows)
    row = jnp.concatenate([-jnp.ones((N_META,), jnp.int32), tok_row.astype(jnp.int32)]).astype(jnp.float32)
    col = jnp.concatenate([jnp.arange(N_META, dtype=jnp.int32), tok_col.astype(jnp.int32)]).astype(jnp.float32)
    inv_freq = ROPE_THETA ** (-jnp.arange(ROPE_FREQS, dtype=jnp.float32) / ROPE_FREQS)
    ang = jnp.concatenate([row[:, None] * inv_freq, col[:, None] * inv_freq], axis=-1)
    ang = jnp.concatenate([ang, ang], axis=-1)
    return jnp.cos(ang), jnp.sin(ang)


def apply_rope(x, cos, sin):
    xf = x.astype(jnp.float32)
    x1, x2 = xf[..., :ATTN_HEAD_DIM // 2], xf[..., ATTN_HEAD_DIM // 2:]
    rot = jnp.concatenate([-x2, x1], axis=-1)
    return (xf * cos[:, None] + rot * sin[:, None]).astype(x.dtype)


def attention_mixer(h, w_qkv, q_norm_g, k_norm_g, w_o):
    b, L, _ = h.shape
    n_tokens = L - N_META
    qkv = h @ w_qkv
    q = qkv[..., :Q_WIDTH].reshape(b, L, N_Q_HEADS, ATTN_HEAD_DIM)
    k = qkv[..., Q_WIDTH:Q_WIDTH + KV_WIDTH].reshape(b, L, N_KV_HEADS, ATTN_HEAD_DIM)
    v = qkv[..., Q_WIDTH + KV_WIDTH:].reshape(b, L, N_KV_HEADS, ATTN_HEAD_DIM)
    cos, sin = axial_rope_tables(n_tokens)
    q = apply_rope(rms_norm(q, q_norm_g), cos, sin) * ATTN_SCALE
    k = apply_rope(rms_norm(k, k_norm_g), cos, sin)
    q = q.reshape(b, L, N_KV_HEADS, KV_REP, ATTN_HEAD_DIM)

    def attend(qb):
        s = jnp.einsum('bqgrd,bkgd->bgrqk', qb, k).astype(jnp.float32)
        p = jax.nn.softmax(s, axis=-1).astype(v.dtype)
        return jnp.einsum('bgrqk,bkgd->bqgrd', p, v)

    o_meta = attend(q[:, :N_META]).reshape(b, N_META, Q_WIDTH)
    qb = q[:, N_META:].reshape(b, n_tokens // Q_BLOCK, Q_BLOCK, N_KV_HEADS, KV_REP, ATTN_HEAD_DIM)
    o_tok = lax.map(attend, jnp.moveaxis(qb, 1, 0))
    o_tok = jnp.moveaxis(o_tok, 0, 1).reshape(b, n_tokens, Q_WIDTH)
    o = jnp.concatenate([o_meta, o_tok], axis=1)
    return o @ w_o


def peer(h, w_q, sub_keys, u, v):
    b, L, D = h.shape
    T = b * L
    n_blk = -(-T // PEER_BLOCK)
    xf = jnp.pad(h.reshape(T, D), ((0, n_blk * PEER_BLOCK - T), (0, 0))).reshape(n_blk, PEER_BLOCK, D)

    def retrieve(xb):
        q = (xb @ w_q).reshape(PEER_BLOCK, PEER_HEADS, 2, PEER_HALF)
        s = jnp.einsum('thcd,hckd->thck', q, sub_keys).astype(jnp.float32)
        sv, si = lax.top_k(s, PEER_TOPK)
        cand = (sv[..., 0, :, None] + sv[..., 1, None, :]).reshape(PEER_BLOCK, PEER_HEADS, PEER_TOPK * PEER_TOPK)
        cand_idx = (si[..., 0, :, None] * N_KEYS + si[..., 1, None, :]).reshape(PEER_BLOCK, PEER_HEADS, PEER_TOPK * PEER_TOPK)
        top, pos = lax.top_k(cand, PEER_TOPK)
        eidx = jnp.take_along_axis(cand_idx, pos, axis=-1)
        g = jax.nn.softmax(top, axis=-1).astype(xb.dtype)
        act = jax.nn.gelu(jnp.einsum('thkd,td->thk', u[eidx], xb), approximate=False)
        return jnp.einsum('thk,thkd->td', g * act, v[eidx])

    out = lax.map(retrieve, xf).reshape(n_blk * PEER_BLOCK, D)[:T]
    return out.reshape(b, L, D)


def trunk(x, meta_tokens, norm_mix_g, norm_ffn_g, ssd_w_in, ssd_conv_w, ssd_conv_b, ssd_dt_bias, ssd_a_log,
          ssd_d_skip, ssd_gate_norm_g, ssd_w_out, attn_w_qkv, attn_q_norm_g, attn_k_norm_g, attn_w_o,
          peer_w_q, peer_sub_keys, peer_u, peer_v):
    b = x.shape[0]
    meta = jnp.broadcast_to(meta_tokens[None].astype(x.dtype), (b, N_META, D_MODEL))
    h = jnp.concatenate([meta, x], axis=1)
    for i in range(DEPTH):
        hn = rms_norm(h, norm_mix_g[i])
        j = i // N_MIXERS
        if i % N_MIXERS == 0:
            h = h + ssd_mixer(hn, ssd_w_in[j], ssd_conv_w[j], ssd_conv_b[j], ssd_dt_bias[j], ssd_a_log[j],
                              ssd_d_skip[j], ssd_gate_norm_g[j], ssd_w_out[j])
        else:
            h = h + attention_mixer(hn, attn_w_qkv[j], attn_q_norm_g[j], attn_k_norm_g[j], attn_w_o[j])
        h = h + peer(rms_norm(h, norm_ffn_g[i]), peer_w_q[i], peer_sub_keys[i], peer_u[i], peer_v[i])
    return h[:, N_META:]


def setup_inputs(seed: int = 0) -> dict:
    key = jax.random.key(seed)
    ks = jax.random.split(key, 24)
    nrm = jax.random.normal
    dt0 = jnp.exp(jax.random.uniform(ks[7], (N_SSD_LAYERS, 2, SSD_HEADS), minval=float(np.log(1e-3)), maxval=float(np.log(1e-1))))
    return {
        'x_prompt': nrm(ks[0], (BATCH, SEQ, D_MODEL), jnp.float32),
        'x_sample': nrm(ks[1], (DEC_BATCH, DEC_SEQ, D_MODEL), jnp.float32),
        'meta_tokens': nrm(ks[2], (N_META, D_MODEL), jnp.float32),
        'norm_mix_g': 1.0 + 0.02 * nrm(ks[3], (DEPTH, D_MODEL), jnp.float32),
        'norm_ffn_g': 1.0 + 0.02 * nrm(ks[4], (DEPTH, D_MODEL), jnp.float32),
        'ssd_w_in': nrm(ks[5], (N_SSD_LAYERS, D_MODEL, SSD_IN_DIM), jnp.float32) * D_MODEL ** -0.5,
        'ssd_conv_w': nrm(ks[6], (N_SSD_LAYERS, CONV_W, CONV_DIM), jnp.float32) * CONV_W ** -0.5,
        'ssd_conv_b': 0.02 * nrm(ks[8], (N_SSD_LAYERS, CONV_DIM), jnp.float32),
        'ssd_dt_bias': dt0 + jnp.log(-jnp.expm1(-dt0)),
        'ssd_a_log': jnp.log(jax.random.uniform(ks[9], (N_SSD_LAYERS, 2, SSD_HEADS), minval=1.0, maxval=16.0)),
        'ssd_d_skip': 1.0 + 0.02 * nrm(ks[10], (N_SSD_LAYERS, SSD_HEADS), jnp.float32),
        'ssd_gate_norm_g': 1.0 + 0.02 * nrm(ks[11], (N_SSD_LAYERS, D_INNER), jnp.float32),
        'ssd_w_out': nrm(ks[12], (N_SSD_LAYERS, D_INNER, D_MODEL), jnp.float32) * D_INNER ** -0.5,
        'attn_w_qkv': nrm(ks[13], (N_ATTN_LAYERS, D_MODEL, QKV_DIM), jnp.float32) * D_MODEL ** -0.5,
        'attn_q_norm_g': 1.0 + 0.02 * nrm(ks[14], (N_ATTN_LAYERS, ATTN_HEAD_DIM), jnp.float32),
        'attn_k_norm_g': 1.0 + 0.02 * nrm(ks[15], (N_ATTN_LAYERS, ATTN_HEAD_DIM), jnp.float32),
        'attn_w_o': nrm(ks[16], (N_ATTN_LAYERS, Q_WIDTH, D_MODEL), jnp.float32) * Q_WIDTH ** -0.5,
        'peer_w_q': nrm(ks[17], (DEPTH, D_MODEL, PEER_HEADS * PEER_KEY_DIM), jnp.float32) * D_MODEL ** -0.5,
        'peer_sub_keys': nrm(ks[18], (DEPTH, PEER_HEADS, 2, N_KEYS, PEER_HALF), jnp.float32) * PEER_HALF ** -0.5,
        'peer_u': nrm(ks[19], (DEPTH, N_EXPERTS, D_MODEL), jnp.float32) * D_MODEL ** -0.5,
        'peer_v': nrm(ks[20], (DEPTH, N_EXPERTS, D_MODEL), jnp.float32) * D_MODEL ** -0.5,
    }


def reference(x_prompt, x_sample, meta_tokens, norm_mix_g, norm_ffn_g, ssd_w_in, ssd_conv_w, ssd_conv_b,
              ssd_dt_bias, ssd_a_log, ssd_d_skip, ssd_gate_norm_g, ssd_w_out, attn_w_qkv, attn_q_norm_g,
              attn_k_norm_g, attn_w_o, peer_w_q, peer_sub_keys, peer_u, peer_v):
    weights = (meta_tokens, norm_mix_g, norm_ffn_g, ssd_w_in, ssd_conv_w, ssd_conv_b, ssd_dt_bias, ssd_a_log,
               ssd_d_skip, ssd_gate_norm_g, ssd_w_out, attn_w_qkv, attn_q_norm_g, attn_k_norm_g, attn_w_o,
               peer_w_q, peer_sub_keys, peer_u, peer_v)
    y_prompt = trunk(x_prompt, *weights)
    y_sample = trunk(x_sample, *weights)
    return (y_prompt, y_sample)
```

```python
from contextlib import ExitStack
import numpy as np
import concourse.bass as bass
import concourse.mybir as mybir

F32 = mybir.dt.float32
BF16 = mybir.dt.bfloat16
AF = mybir.ActivationFunctionType
ALU = mybir.AluOpType
AX = mybir.AxisListType

ENGS = ("pe", "act", "dve", "pool", "sp")
SAME_ENGINE_SYNC = ("act", "dve", "pool")


class Buf:
    __slots__ = ("name", "w", "r")

    def __init__(self, name):
        self.name = name
        self.w = []
        self.r = []


class Prog:
    def __init__(self, nc, es):
        self.nc = nc
        self.es = es
        self.ops = {e: [] for e in ENGS}
        self.sems = {}
        self.cnt = {}
        self.seen = {e: {} for e in ENGS}
        self.nops = 0

    def sem(self, name):
        if name not in self.sems:
            self.sems[name] = self.es.enter_context(self.nc.semaphore(name))
            self.cnt[name] = 0
        return name

    def sb(self, name, shape, dt):
        return self.es.enter_context(self.nc.sbuf_tensor(name, list(shape), dt))

    def ps(self, name, shape, dt):
        return self.es.enter_context(self.nc.psum_tensor(name, list(shape), dt))

    def _emit(self, eng, fn, reads, writes, semname, inc):
        waits = {}
        for b in reads:
            for (s, v) in b.w:
                if waits.get(s, 0) < v:
                    waits[s] = v
        for b in writes:
            for (s, v) in b.w:
                if waits.get(s, 0) < v:
                    waits[s] = v
            for (s, v) in b.r:
                if waits.get(s, 0) < v:
                    waits[s] = v
        own = "E_" + eng
        wl = []
        seen = self.seen[eng]
        for s, v in waits.items():
            if s == own and eng not in SAME_ENGINE_SYNC:
                continue
            if seen.get(s, 0) >= v:
                continue
            seen[s] = v
            wl.append((s, v))
        self.sem(semname)
        self.cnt[semname] += inc
        ev = (semname, self.cnt[semname])
        self.ops[eng].append((wl, fn, semname, inc))
        wset = set(id(b) for b in writes)
        for b in writes:
            b.w = [ev]
            b.r = []
        for b in reads:
            if id(b) not in wset:
                b.r.append(ev)
                if len(b.r) > 64:
                    m = {}
                    for (s, v) in b.r:
                        if m.get(s, 0) < v:
                            m[s] = v
                    b.r = list(m.items())
        self.nops += 1
        return ev

    def op(self, eng, fn, reads=(), writes=()):
        return self._emit(eng, fn, reads, writes, "E_" + eng, 1)

    def dma(self, queue, semname, out, in_, reads=(), writes=()):
        def fn(e):
            return e.dma_start(out=out, in_=in_)
        return self._emit(queue, fn, reads, writes, semname, 16)

    def finalize(self, bufs, semnames):
        evs = [(s, self.cnt[s]) for s in semnames if s in self.cnt]
        for b in bufs:
            b.w = list(evs)
            b.r = []

    def final_wait(self, eng, bufs):
        waits = {}
        for b in bufs:
            for (s, v) in b.w + b.r:
                if waits.get(s, 0) < v:
                    waits[s] = v
        self.ops[eng].append((list(waits.items()), None, None, 0))

    def barrier(self):
        evs = [(s, v) for s, v in self.cnt.items() if v > 0]
        for eng in ENGS:
            wl = []
            for (s, v) in evs:
                if self.seen[eng].get(s, 0) < v:
                    self.seen[eng][s] = v
                    wl.append((s, v))
            self.ops[eng].append((wl, None, None, 0))

    def build(self):
        nc = self.nc
        engmap = {"pe": "tensor", "act": "scalar", "dve": "vector", "pool": "gpsimd", "sp": "sync"}
        with nc.Block() as block:
            for eng in ENGS:
                ops = self.ops[eng]
                sems = self.sems

                def body(e, ops=ops):
                    for (wl, fn, semname, inc) in ops:
                        for (s, v) in wl:
                            e.wait_ge(sems[s], v)
                        if fn is not None:
                            ins = fn(e)
                            ins.then_inc(sems[semname], inc)

                getattr(block, engmap[eng])(body)
        self.ops = {e: [] for e in ENGS}

from concourse.bass_utils import run_bass_kernel_spmd

D = 1024
DIN = 2048
NCH = 3072
NZD = 2112
EPS = 1e-6
UID = [0]
LEVEL = 99


class H:
    def __init__(self, p):
        self.p = p

    def mm(self, out, lhsT, rhs, start, stop, R, W):
        self.p.op("pe", lambda e: e.matmul(out, lhsT=lhsT, rhs=rhs, start=start, stop=stop), R, W)

    def mm_acc(self, out, lhsT, rhs, stop, R, W):
        self.p.op("pe", lambda e: e.matmul(out, lhsT=lhsT, rhs=rhs, start=False, stop=stop, skip_group_check=True), R, W)

    def tr(self, out, in_, ident, R, W):
        self.p.op("pe", lambda e: e.transpose(out, in_, ident), R, W)

    def act(self, out, in_, func, R, W, bias=None, scale=None, accum_out=None, eng="act"):
        kw = {}
        if bias is not None:
            kw["bias"] = bias
        if scale is not None:
            kw["scale"] = scale
        if accum_out is not None:
            kw["accum_out"] = accum_out
        self.p.op("act", lambda e: e.activation(out, in_, func, **kw), R, W)

    def tt(self, eng, out, in0, in1, op, R, W):
        self.p.op(eng, lambda e: e.tensor_tensor(out, in0, in1, op), R, W)

    def ts(self, eng, out, in0, s1, s2, op0, op1, R, W):
        if op1 is None:
            self.p.op(eng, lambda e: e.tensor_scalar(out, in0, s1, None, op0), R, W)
        else:
            self.p.op(eng, lambda e: e.tensor_scalar(out, in0, s1, s2, op0, op1), R, W)

    def stt(self, out, in0, scalar, in1, op0, op1, R, W):
        self.p.op("dve", lambda e: e.scalar_tensor_tensor(out, in0, scalar, in1, op0, op1), R, W)

    def cp(self, eng, out, in_, R, W):
        if eng == "act":
            self.p.op("act", lambda e: e.copy(out, in_), R, W)
        else:
            self.p.op(eng, lambda e: e.tensor_copy(out, in_), R, W)

    def memset(self, eng, ap, val, W):
        self.p.op(eng, lambda e: e.memset(ap, val), (), W)

    def recip(self, out, in_, R, W):
        self.p.op("dve", lambda e: e.reciprocal(out, in_), R, W)

    def reduce(self, eng, out, in_, op, R, W):
        self.p.op(eng, lambda e: e.tensor_reduce(out, in_, AX.X, op), R, W)


def rmsnorm_tile(h, x_ap, g_ap, out_ap, width, scr, R, W, tag):
    junk, ssq, rstd, B = scr["junk"], scr["ssq"], scr["rstd"], scr["B"]
    h.act(junk[:, :width], x_ap, AF.Square, R, [B], accum_out=ssq[:, 0:1])
    h.ts("dve", rstd[:, 0:1], ssq[:, 0:1], 1.0 / width, EPS, ALU.mult, ALU.add, [B], [B])
    h.act(rstd[:, 0:1], rstd[:, 0:1], AF.Sqrt, [B], [B])
    h.recip(rstd[:, 0:1], rstd[:, 0:1], [B], [B])
    h.stt(out_ap, x_ap, rstd[:, 0:1], g_ap, ALU.mult, ALU.mult, list(R) + [B], W)


def phase1(nc, p, h, NT, dr, consts):
    es = ExitStack()
    with es:
        UID[0] += 1
        uid = UID[0]

        def sb(name, shape, dt):
            return es.enter_context(nc.sbuf_tensor(f"{name}_u{uid}", list(shape), dt))

        def pst(name, shape, dt):
            return es.enter_context(nc.psum_tensor(f"{name}_u{uid}", list(shape), dt))
        ident = consts["ident_bf"]
        win = sb("win", [128, 8, 5184], BF16)
        Bwin = Buf("win")
        for kc in range(8):
            p.dma("pool", "d_win", win[:, kc, :], dr["w_in"][kc * 128:(kc + 1) * 128, :], (), [Bwin])
        cw = sb("cw", [128, 24, 5], F32)
        cb = sb("cbias", [128, 24], F32)
        Bcw = Buf("cw")
        p.dma("sp", "d_cw", cw[:], dr["conv_w"][:, :, :], (), [Bcw])
        p.dma("sp", "d_cw", cb[:], dr["conv_b"][:, :], (), [Bcw])
        dg = sb("dg", [128, 24, 5, 128], BF16)
        Bdg = Buf("dg")
        for blk in range(24):
            for k in range(5):
                h.ts("pool" if (blk + k) % 2 else "dve", dg[:, blk, k, :], consts["ident_f"][:], cw[:, blk, k:k + 1], None, ALU.mult, None,
                     [Bcw, consts["B"]], [Bdg])
        gmix = sb("gmix", [128, D], F32)
        Bg = Buf("gmix")
        p.dma("sp", "d_g", gmix[:], dr["g_mix0"].partition_broadcast(128), (), [Bg])
        xt = [sb(f"xt{i}", [128, D], F32) for i in range(2)]
        Bxt = [Buf(f"xt{i}") for i in range(2)]
        scr = {"junk": sb("junk", [128, D], F32), "ssq": sb("ssq", [128, 1], F32), "rstd": sb("rstd", [128, 1], F32), "B": Buf("nscr")}
        xn = sb("xn", [128, D], BF16)
        Bxn = Buf("xn")
        xnT = sb("xnT", [128, 8, 128], BF16)
        BxnT = Buf("xnT")
        slots = [sb(f"slot{i}", [128, 24, 132], BF16) for i in range(3)]
        Bsl = [Buf(f"slot{i}") for i in range(3)]
        zst = [sb(f"zst{i}", [128, NZD], F32) for i in range(2)]
        Bzst = [Buf(f"zst{i}") for i in range(2)]
        cst = [sb(f"cst{i}", [128, 24, 128], BF16) for i in range(2)]
        Bcst = [Buf(f"cst{i}") for i in range(2)]
        ps_t = pst("ps_t", [128, 8, 128], BF16)
        Bps_t = Buf("ps_t")
        ps_a = [pst(f"ps_a{i}", [128, 512], F32) for i in range(3)]
        Bps_a = [Buf(f"ps_a{i}") for i in range(3)]
        ps_c = [pst(f"ps_c{i}", [128, 4, 128], F32) for i in range(2)]
        Bps_c = [Buf(f"ps_c{i}") for i in range(2)]
        pa = 0
        pc = 0
        h.memset("pool", slots[0][:, :, 0:2], 0.0, [Bsl[0]])

        def conv_tile(c):
            nonlocal pc
            s = slots[c % 3]
            for q in range(6):
                pb = ps_c[pc % 2]
                Bp = Bps_c[pc % 2]
                pc += 1
                for bb in range(4):
                    blk = q * 4 + bb
                    for k in range(5):
                        h.mm(pb[:, bb, :], dg[:, blk, k, :], s[:, blk, k:k + 128], k == 0, k == 4, [Bdg, Bsl[c % 3]], [Bp])
                for bb in range(4):
                    blk = q * 4 + bb
                    h.act(cst[c % 2][:, blk, :], pb[:, bb, :], AF.Silu, [Bp, Bcw], [Bcst[c % 2]], bias=cb[:, blk:blk + 1])
            p.dma("sp", f"d_cst{c % 2}", dr["xbca"][c], cst[c % 2][:].rearrange("p a b -> p (a b)"), [Bcst[c % 2]], [dr["B_xbca"]])

        p.dma("sp", "d_xt0", xt[0][:], dr["x"][0], (), [Bxt[0]])
        for c in range(NT):
            if c + 1 < NT:
                p.dma("sp", f"d_xt{(c + 1) % 2}", xt[(c + 1) % 2][:], dr["x"][c + 1], (), [Bxt[(c + 1) % 2]])
            rmsnorm_tile(h, xt[c % 2][:], gmix[:], xn[:], D, scr, [Bxt[c % 2], Bg], [Bxn], "p1")
            for kc in range(8):
                h.tr(ps_t[:, kc, :], xn[:, kc * 128:(kc + 1) * 128], ident[:], [Bxn, consts["B"]], [Bps_t])
            h.cp("act", xnT[:].rearrange("p a b -> p (a b)"), ps_t[:].rearrange("p a b -> p (a b)"), [Bps_t], [BxnT])
            s = slots[c % 3]
            for q in range(6):
                pb = ps_c[pc % 2]
                Bp = Bps_c[pc % 2]
                pc += 1
                for bb in range(4):
                    blk = q * 4 + bb
                    col = DIN + blk * 128
                    for kc in range(8):
                        h.mm(pb[:, bb, :], win[:, kc, col:col + 128], xnT[:, kc, :], kc == 0, kc == 7, [Bwin, BxnT], [Bp])
                h.cp("dve" if q % 2 else "act", s[:, q * 4:(q + 1) * 4, 2:130], pb[:], [Bp], [Bsl[c % 3]])
            zs = zst[c % 2]
            for nb in range(5):
                pb = ps_a[pa % 3]
                Bp = Bps_a[pa % 3]
                pa += 1
                if nb < 4:
                    c0, w = nb * 512, 512
                    d0 = c0
                else:
                    c0, w = DIN + NCH, 64
                    d0 = 2048
                for kc in range(8):
                    h.mm(pb[:, :w], xnT[:, kc, :], win[:, kc, c0:c0 + w], kc == 0, kc == 7, [Bwin, BxnT], [Bp])
                h.cp("dve", zs[:, d0:d0 + w], pb[:, :w], [Bp], [Bzst[c % 2]])
            p.dma("sp", f"d_zst{c % 2}", dr["zdt"][c], zs[:], [Bzst[c % 2]], [dr["B_zdt"]])
            if c > 0:
                sp_ = slots[(c - 1) % 3]
                h.cp("pool", sp_[:, :, 130:132], s[:, :, 2:4], [Bsl[c % 3]], [Bsl[(c - 1) % 3]])
                h.cp("pool", s[:, :, 0:2], sp_[:, :, 128:130], [Bsl[(c - 1) % 3]], [Bsl[c % 3]])
                conv_tile(c - 1)
        h.memset("pool", slots[(NT - 1) % 3][:, :, 130:132], 0.0, [Bsl[(NT - 1) % 3]])
        conv_tile(NT - 1)
        p.finalize([dr["B_xbca"]], ["d_cst0", "d_cst1"])
        p.finalize([dr["B_zdt"]], ["d_zst0", "d_zst1"])


def bc3(ap2, n):
    return ap2.unsqueeze(2).broadcast_to([ap2.shape[0], ap2.shape[1], n])


def phase2(nc, p, h, NT, dr, consts, dirn, final):
    es = ExitStack()
    with es:
        UID[0] += 1
        uid = UID[0]

        def sb(name, shape, dt):
            return es.enter_context(nc.sbuf_tensor(f"{name}_u{uid}", list(shape), dt))

        def pst(name, shape, dt):
            return es.enter_context(nc.psum_tensor(f"{name}_u{uid}", list(shape), dt))
        p.barrier()
        CB = consts["B"]
        ident = consts["ident_bf"]
        ident_f = consts["ident_f"]
        tri = consts["tri"][dirn]
        mneg = consts["mneg"][dirn]
        ones = consts["ones"]
        d0 = dirn * 32
        cvec = sb("cvec", [128, 64 + 64 + 32], F32)
        Bcv = Buf("cvec")
        p.dma("sp", "d_c2", cvec[:, 0:64], dr["dt_bias"].partition_broadcast(128), (), [Bcv])
        p.dma("sp", "d_c2", cvec[:, 64:128], dr["a_log"].partition_broadcast(128), (), [Bcv])
        p.dma("sp", "d_c2", cvec[:, 128:160], dr["d_skip"].partition_broadcast(128), (), [Bcv])
        valid = sb("valid", [128, NT], F32)
        p.dma("sp", "d_c2", valid[:], dr["valid"][:, :], (), [Bcv])
        h.act(cvec[:, 64:128], cvec[:, 64:128], AF.Exp, [Bcv], [Bcv])
        h.ts("dve", cvec[:, 64:128], cvec[:, 64:128], -1.0, None, ALU.mult, None, [Bcv], [Bcv])
        dtb = cvec[:, d0:d0 + 32]
        a_d = cvec[:, 64 + d0:64 + d0 + 32]
        dsk = cvec[:, 128:160]
        if final:
            wout = sb("wout", [128, 16, D], BF16)
            Bwout = Buf("wout")
            for kc in range(16):
                p.dma("pool", "d_wout", wout[:, kc, :], dr["w_out"][kc * 128:(kc + 1) * 128, :], (), [Bwout])
            gg = sb("gg", [128, DIN], F32)
            p.dma("sp", "d_c2", gg[:], dr["gate_g"].partition_broadcast(128), (), [Bcv])
            zt = [sb(f"zt{i}", [128, NZD], F32) for i in range(2)]
            yb = [sb(f"yb{i}", [128, DIN], F32) for i in range(2)]
            Byb = [Buf(f"yb{i}") for i in range(2)]
            xt = [sb(f"xt{i}", [128, D], F32) for i in range(2)]
            Bxt = [Buf(f"xt{i}") for i in range(2)]
            szt = sb("szt", [128, DIN], F32)
            Bszt = Buf("szt")
            yn = sb("yn", [128, DIN], BF16)
            Byn = Buf("yn")
            ynT = sb("ynT", [128, 16, 128], BF16)
            BynT = Buf("ynT")
            h1t = [sb(f"h1t{i}", [128, D], F32) for i in range(2)]
            Bh1t = [Buf(f"h1t{i}") for i in range(2)]
            scr = {"junk": sb("junk2", [128, DIN], BF16), "ssq": sb("ssq2", [128, 1], F32), "rstd": sb("rstd2", [128, 1], F32), "B": Buf("nscr2")}
        else:
            zt = [sb(f"zt{i}", [128, 64], F32) for i in range(2)]
        Bzt = [Buf(f"zt{i}") for i in range(2)]
        xa = [sb(f"xa{i}", [128, 24, 128], BF16) for i in range(2)]
        Bxa = [Buf(f"xa{i}") for i in range(2)]
        sm = sb("sm", [128, 16, 32], F32)
        Bsm = Buf("sm")
        T1, EE, DT, ADT, CS, TOT, NCS, ECS, DTE, SC2, DEC = [sm[:, i, :] for i in range(11)]
        adtb = sb("adtb", [128, 32, 128], F32)
        Badtb = Buf("adtb")
        xstok = sb("xstok", [128, 32, 64], BF16)
        Bxstok = Buf("xstok")
        xdt = sb("xdt", [128, 32, 64], BF16)
        Bxdt = Buf("xdt")
        xdtd = sb("xdtd", [128, 32, 64], BF16)
        Bxdtd = Buf("xdtd")
        Btok = sb("Btok", [128, 4, 128], BF16)
        BBtok = Buf("Btok")
        cbT = sb("cbT", [128, 4, 128], BF16)
        BcbT = Buf("cbT")
        Eh = [sb(f"Eh{i}", [128, 128], BF16) for i in range(4)]
        BEh = [Buf(f"Eh{i}") for i in range(4)]
        wT = [sb(f"wT{i}", [128, 128], BF16) for i in range(4)]
        BwT = [Buf(f"wT{i}") for i in range(4)]
        y = [sb(f"y{i}", [128, DIN], F32) for i in range(2)]
        By = [Buf(f"y{i}") for i in range(2)]
        ytmp = sb("ytmp", [128, 8, 64], F32)
        Bytmp = Buf("ytmp")
        hst = sb("hst", [128, 32, 64], F32)
        Bhst = Buf("hst")
        hbf = sb("hbf", [128, 32, 64], BF16)
        Bhbf = Buf("hbf")
        h.memset("pool", hst[:], 0.0, [Bhst])
        h.memset("pool", hbf[:], 0.0, [Bhbf])
        ps_s = pst("ps_s", [128, 512], F32)
        Bps_s = Buf("ps_s")
        ps_x = [pst(f"ps_x{i}", [128, 8, 128], BF16) for i in range(2)]
        Bps_x = [Buf(f"ps_x{i}") for i in range(2)]
        ps_sg = [pst(f"ps_sg{i}", [128, 4, 128], F32) for i in range(2)]
        Bps_sg = [Buf(f"ps_sg{i}") for i in range(2)]
        ps_y = pst("ps_y", [128, 8, 64], F32)
        Bps_y = Buf("ps_y")
        ps_r = [pst(f"ps_r{i}", [128, 512], F32) for i in range(2)]
        Bps_r = [Buf(f"ps_r{i}") for i in range(2)]
        rr = 0
        order = list(range(NT - 1, -1, -1)) if dirn == 1 else list(range(NT))

        def loads(i):
            c = order[i]
            sl = i % 2
            p.dma("sp", f"d_xa{sl}", xa[sl][:].rearrange("p a b -> p (a b)"), dr["xbca"][c], [dr["B_xbca"]], [Bxa[sl]])
            if final:
                p.dma("sp", f"d_zt{sl}", zt[sl][:], dr["zdt"][c], [dr["B_zdt"]], [Bzt[sl]])
                p.dma("sp", f"d_yb{sl}", yb[sl][:], dr["ybwd"][c], [dr["B_ybwd"]], [Byb[sl]])
                p.dma("sp", f"d_xt2{sl}", xt[sl][:], dr["x"][c], (), [Bxt[sl]])
            else:
                p.dma("sp", f"d_zt{sl}", zt[sl][:], dr["zdt"][c][:, DIN:DIN + 64], [dr["B_zdt"]], [Bzt[sl]])

        loads(0)
        for i, c in enumerate(order):
            sl = i % 2
            if i + 1 < NT:
                loads(i + 1)
            X = xa[sl]
            BX = Bxa[sl]
            dtr = zt[sl][:, DIN + d0:DIN + d0 + 32] if final else zt[sl][:, d0:d0 + 32]
            h.tt("dve", T1, dtr, dtb, ALU.add, [Bzt[sl], Bcv], [Bsm])
            h.act(EE, T1, AF.Exp, [Bsm], [Bsm])
            h.act(EE, EE, AF.Ln, [Bsm], [Bsm], bias=1.0)
            h.ts("dve", DT, EE, valid[:, c:c + 1], None, ALU.mult, None, [Bsm, Bcv], [Bsm])
            h.tt("dve", ADT, DT, a_d, ALU.mult, [Bsm, Bcv], [Bsm])
            h.cp("pool", adtb[:], bc3(ADT, 128), [Bsm], [Badtb])
            if LEVEL <= 1:
                continue
            h.mm(ps_s[:, 0:32], tri[:], ADT, True, True, [Bsm, CB], [Bps_s])
            h.mm(ps_s[:, 32:64], ones[:], ADT, True, True, [Bsm, CB], [Bps_s])
            h.cp("dve", sm[:, 4:6, :].rearrange("p a b -> p (a b)"), ps_s[:, 0:64], [Bps_s], [Bsm])
            h.ts("dve", NCS, CS, -1.0, None, ALU.mult, None, [Bsm], [Bsm])
            h.act(ECS, CS, AF.Exp, [Bsm], [Bsm])
            h.tt("dve", DTE, TOT, CS, ALU.subtract, [Bsm], [Bsm])
            h.act(DTE, DTE, AF.Exp, [Bsm], [Bsm])
            h.tt("dve", SC2, DT, DTE, ALU.mult, [Bsm], [Bsm])
            h.act(DEC, TOT, AF.Exp, [Bsm], [Bsm])
            if LEVEL <= 2:
                continue
            for half in range(2):
                for b in range(8):
                    h.tr(ps_x[half][:, b, :], X[:, half * 8 + b, :], ident[:], [BX, CB], [Bps_x[half]])
                hs = slice(half * 16, half * 16 + 16)
                pv = ps_x[half][:].rearrange("p a (t q) -> p (a t) q", q=64)
                h.cp("act", xstok[:, hs, :], pv, [Bps_x[half]], [Bxstok])
                h.tt("dve", xdt[:, hs, :], xstok[:, hs, :], bc3(DT[:, hs], 64), ALU.mult, [Bxstok, Bsm], [Bxdt])
                h.tt("dve", xdtd[:, hs, :], xstok[:, hs, :], bc3(SC2[:, hs], 64), ALU.mult, [Bxstok, Bsm], [Bxdtd])
            for g in range(4):
                h.tr(ps_x[0][:, g, :], X[:, 16 + g, :], ident[:], [BX, CB], [Bps_x[0]])
            h.cp("act", Btok[:], ps_x[0][:, 0:4, :], [Bps_x[0]], [BBtok])
            if LEVEL <= 3:
                continue
            pr = ps_r[rr % 2]
            Bpr = Bps_r[rr % 2]
            rr += 1
            prv = pr[:].rearrange("p (a b) -> p a b", b=128)
            for g in range(4):
                h.mm(prv[:, g, :], X[:, 16 + g, :], X[:, 20 + g, :], True, True, [BX], [Bpr])
            h.cp("act", cbT[:], prv, [Bpr], [BcbT])
            yy = y[i % 2]
            Byy = By[i % 2]
            if LEVEL <= 4:
                continue
            for g in range(4):
                for r4 in range(2):
                    sg = ps_sg[(g * 2 + r4) % 2]
                    Bsg = Bps_sg[(g * 2 + r4) % 2]
                    for j in range(4):
                        hh = g * 8 + r4 * 4 + j
                        h.mm(sg[:, j, :], adtb[:, hh, :], tri[:], True, False, [Badtb, CB], [Bsg])
                        h.mm(sg[:, j, :], ident_f[:], mneg[:], False, True, [CB], [Bsg])
                    for j in range(4):
                        hh = g * 8 + r4 * 4 + j
                        r = r4 * 4 + j
                        h.act(Eh[j][:], sg[:, j, :], AF.Exp, [Bsg, Bsm], [BEh[j]], bias=NCS[:, hh:hh + 1])
                        h.tt("dve", wT[j][:], Eh[j][:], cbT[:, g, :], ALU.mult, [BEh[j], BcbT], [BwT[j]])
                        h.mm(ps_y[:, r, :], wT[j][:], xdt[:, hh, :], True, True, [BwT[j], Bxdt], [Bps_y])
                pr = ps_r[rr % 2]
                Bpr = Bps_r[rr % 2]
                rr += 1
                h.mm(pr[:], X[:, 20 + g, :], hbf[:, g * 8:(g + 1) * 8, :].rearrange("p a b -> p (a b)"), True, True, [BX, Bhbf], [Bpr])
                gs = slice(g * 8, g * 8 + 8)
                h.tt("dve", ytmp[:], pr[:].rearrange("p (a b) -> p a b", b=64), bc3(ECS[:, gs], 64), ALU.mult, [Bpr, Bsm], [Bytmp])
                h.tt("dve", yy[:, g * 512:(g + 1) * 512].rearrange("p (a b) -> p a b", b=64), ytmp[:], ps_y[:], ALU.add, [Bytmp, Bps_y], [Byy])
                pr = ps_r[rr % 2]
                Bpr = Bps_r[rr % 2]
                rr += 1
                h.mm(pr[:], Btok[:, g, :], xdtd[:, gs, :].rearrange("p a b -> p (a b)"), True, True, [BBtok, Bxdtd], [Bpr])
                h.tt("pool", hst[:, gs, :], hst[:, gs, :], bc3(DEC[:, gs], 64), ALU.mult, [Bsm], [Bhst])
                h.tt("dve", hst[:, gs, :], hst[:, gs, :], pr[:].rearrange("p (a b) -> p a b", b=64), ALU.add, [Bpr], [Bhst])
                h.cp("pool", hbf[:, gs, :], hst[:, gs, :], [Bhst], [Bhbf])
            if LEVEL <= 5:
                continue
            if not final:
                p.dma("sp", f"d_y{i % 2}", dr["ybwd"][c], yy[:], [Byy], [dr["B_ybwd"]])
            else:
                h.tt("pool", yy[:], yy[:], yb[sl][:], ALU.add, [Byb[sl]], [Byy])
                h.tt("pool", szt[:].rearrange("p (a b) -> p a b", b=64), xstok[:], bc3(dsk, 64), ALU.mult, [Bxstok, Bcv], [Bszt])
                h.tt("pool", yy[:], yy[:], szt[:], ALU.add, [Bszt], [Byy])
                h.act(szt[:], zt[sl][:, 0:DIN], AF.Silu, [Bzt[sl]], [Bszt])
                h.tt("dve", yy[:], yy[:], szt[:], ALU.mult, [Bszt], [Byy])
                rmsnorm_tile(h, yy[:], gg[:], yn[:], DIN, scr, [Byy, Bcv], [Byn], "gate")
                for half in range(2):
                    for b in range(8):
                        kc = half * 8 + b
                        h.tr(ps_x[half][:, b, :], yn[:, kc * 128:(kc + 1) * 128], ident[:], [Byn, CB], [Bps_x[half]])
                    h.cp("act", ynT[:, half * 8:half * 8 + 8, :], ps_x[half][:], [Bps_x[half]], [BynT])
                ht = h1t[i % 2]
                for nb in range(2):
                    pr = ps_r[rr % 2]
                    Bpr = Bps_r[rr % 2]
                    rr += 1
                    for kc in range(16):
                        h.mm(pr[:], ynT[:, kc, :], wout[:, kc, nb * 512:(nb + 1) * 512], kc == 0, kc == 15, [BynT, Bwout], [Bpr])
                    h.tt("dve", ht[:, nb * 512:(nb + 1) * 512], xt[sl][:, nb * 512:(nb + 1) * 512], pr[:], ALU.add, [Bpr, Bxt[sl]], [Bh1t[i % 2]])
                p.dma("sp", f"d_h1t{i % 2}", dr["h1"][c], ht[:], [Bh1t[i % 2]], [dr["B_h1"]])
        if not final:
            p.finalize([dr["B_ybwd"]], ["d_y0", "d_y1"])
        else:
            p.finalize([dr["B_h1"]], ["d_h1t0", "d_h1t1"])
        p.build()


def phase0(nc, p, h, dr, li):
    es = ExitStack()
    with es:
        UID[0] += 1
        uid = UID[0]
        p.barrier()
        st = [es.enter_context(nc.sbuf_tensor(f"p0s{i}_u{uid}", [128, 8192], F32)) for i in range(2)]
        so = [es.enter_context(nc.sbuf_tensor(f"p0o{i}_u{uid}", [128, 8192], BF16)) for i in range(2)]
        Bst = [Buf("p0s") for _ in range(2)]
        Bso = [Buf("p0o") for _ in range(2)]
        k = 0
        for name in ("uT", "v"):
            for eb in range(16):
                s = k % 2
                p.dma("sp", f"d_p0s{s}", st[s][:], dr[name + "32"][li, eb], (), [Bst[s]])
                if k % 2 == 0:
                    h.cp("act", so[s][:], st[s][:], [Bst[s]], [Bso[s]])
                else:
                    h.cp("pool", so[s][:], st[s][:], [Bst[s]], [Bso[s]])
                p.dma("sp", f"d_p0o{s}", dr[name + "16"][li, eb], so[s][:], [Bso[s]], [dr["B_" + name + "16"]])
                k += 1
        p.finalize([dr["B_uT16"], dr["B_v16"]], ["d_p0o0", "d_p0o1"])
        p.build()


def phase_peer(nc, p, h, NTp, dr, consts, li, src, dst):
    assert NTp % 2 == 0
    es = ExitStack()
    with es:
        UID[0] += 1
        uid = UID[0]

        def sb(name, shape, dt):
            return es.enter_context(nc.sbuf_tensor(f"{name}_u{uid}", list(shape), dt))

        def pst(name, shape, dt):
            return es.enter_context(nc.psum_tensor(f"{name}_u{uid}", list(shape), dt))
        p.barrier()
        CB = consts["B"]
        ident = consts["ident_bf"]
        wq = sb("wq", [128, 8, 2048], BF16)
        Bwq = Buf("wq")
        for kc in range(8):
            p.dma("pool", "d_wq", wq[:, kc, :], dr["peer_wq"][li, kc * 128:(kc + 1) * 128, :], (), [Bwq])
        skT = sb("skT", [128, 16, 128], BF16)
        p.dma("pool", "d_wq", skT[:], dr["peer_skT"][li], (), [Bwq])
        gf = sb("gf", [128, D], F32)
        p.dma("sp", "d_gf", gf[:], dr["g_ffn"][li].partition_broadcast(128), (), [Bwq])
        ub = [sb(f"ub{i}", [128, 8, 1024], BF16) for i in range(2)]
        vb = [sb(f"vb{i}", [128, 8, 1024], BF16) for i in range(2)]
        Bub = [Buf("ub") for _ in range(2)]
        Bvb = [Buf("vb") for _ in range(2)]
        ht = [sb(f"ht{i}", [128, D], F32) for i in range(2)]
        Bht = [Buf("ht") for _ in range(2)]
        ho = [sb(f"ho{i}", [128, D], F32) for i in range(2)]
        Bho = [Buf("ho") for _ in range(2)]
        scr = {"junk": sb("junkp", [128, D], BF16), "ssq": sb("ssqp", [128, 1], F32), "rstd": sb("rstdp", [128, 1], F32), "B": Buf("nscrp")}
        xn = sb("xn", [128, D], BF16)
        Bxn = Buf("xn")
        xnT = sb("xnT", [128, 8, 2, 128], BF16)
        BxnT = Buf("xnT")
        qT = sb("qT", [128, 16, 128], BF16)
        BqT = Buf("qT")
        eall = sb("eall", [128, 16, 128], F32)
        Beall = Buf("eall")
        tmp = sb("tmpk", [128, 256], F32)
        Btmp = Buf("tmpk")
        sv = sb("sv", [128, 16, 16], F32)
        Bsv = Buf("sv")
        sml = sb("sml", [128, 8, 16], F32)
        Bsml = Buf("sml")
        cand = sb("cand", [128, 8, 16, 16], F32)
        Bcand = Buf("cand")
        cv = sb("cv", [128, 8, 16], F32)
        Bcv_ = Buf("cv")
        sv1n = sb("sv1n", [128, 8, 16], F32)
        e1n = [sb(f"e1n{i}", [128, 8, 128], F32) for i in range(2)]
        e2 = [sb(f"e2{i}", [128, 8, 128], F32) for i in range(2)]
        thr = [sb(f"thr{i}", [128, 8], F32) for i in range(2)]
        Bst_ = [Buf("tilestate") for _ in range(2)]
        E = [sb(f"E{i}", [128, 8, 128], F32) for i in range(2)]
        BE = [Buf("E") for _ in range(2)]
        Gh = [sb(f"Gh{i}", [128, 1024], BF16) for i in range(2)]
        BGh = [Buf("Gh") for _ in range(2)]
        ga = sb("ga", [128, 1024], BF16)
        Bga = Buf("ga")
        Gsb = sb("Gsb", [128, 1024], BF16)
        BGsb = Buf("Gsb")
        Asb = sb("Asb", [128, 1024], BF16)
        BAsb = Buf("Asb")
        aT = [sb(f"aT{i}", [128, 8, 128], BF16) for i in range(2)]
        BaT = [Buf("aT") for _ in range(2)]
        ps_o = [[pst(f"ps_o{w}{nb}", [128, 512], F32) for nb in range(2)] for w in range(2)]
        Bps_o = [[Buf("ps_o") for nb in range(2)] for w in range(2)]
        ps_g = [pst(f"ps_g{i}", [128, 4, 128], F32) for i in range(2)]
        Bps_g = [Buf("ps_g") for _ in range(2)]
        ps_a = pst("ps_a", [128, 512], F32)
        Bps_a = Buf("ps_a")
        ps_t = pst("ps_t", [128, 8, 128], BF16)
        Bps_t = Buf("ps_t")
        nld = [0]

        def wload(eb):
            s = nld[0] % 2
            nld[0] += 1
            p.dma("sp", f"d_ub{s}", ub[s][:].rearrange("p a b -> p (a b)"), dr["uT16"][li, eb], [dr["B_uT16"]], [Bub[s]])
            p.dma("sp", f"d_vb{s}", vb[s][:].rearrange("p a b -> p (a b)"), dr["v16"][li, eb], [dr["B_v16"]], [Bvb[s]])
            return s

        def top16(src_ap, dst, Rb, Wb):
            n = 1
            for d_ in src_ap.shape[1:]:
                n *= d_
            h.p.op("dve", lambda e: e.max(out=dst[:, 0:8], in_=src_ap), Rb, Wb)
            tv = tmp[:, 0:n]
            if len(src_ap.shape) == 3:
                tv = tv.rearrange("p (a b) -> p a b", b=src_ap.shape[2])
            h.p.op("dve", lambda e: e.match_replace(out=tv, in_to_replace=dst[:, 0:8], in_values=src_ap, imm_value=-1.0), list(Rb) + list(Wb), [Btmp])
            h.p.op("dve", lambda e: e.max(out=dst[:, 8:16], in_=tv), [Btmp], Wb)

        npass = NTp // 2
        for ps_i in range(npass):
            s0 = wload(0)
            for w in range(2):
                c = ps_i * 2 + w
                p.dma("sp", f"d_ht{w}", ht[w][:], dr[src][c], [dr["B_" + src]], [Bht[w]])
                rmsnorm_tile(h, ht[w][:], gf[:], xn[:], D, scr, [Bht[w], Bwq], [Bxn], "peer")
                for kc in range(8):
                    h.tr(ps_t[:, kc, :], xn[:, kc * 128:(kc + 1) * 128], ident[:], [Bxn, CB], [Bps_t])
                h.cp("act", xnT[:, :, w, :], ps_t[:], [Bps_t], [BxnT])
                for r4 in range(4):
                    pg = ps_g[r4 % 2]
                    Bpg = Bps_g[r4 % 2]
                    for j in range(4):
                        hc = r4 * 4 + j
                        for kc in range(8):
                            h.mm(pg[:, j, :], wq[:, kc, hc * 128:(hc + 1) * 128], xnT[:, kc, w, :], kc == 0, kc == 7, [Bwq, BxnT], [Bpg])
                    h.cp("act", qT[:, r4 * 4:r4 * 4 + 4, :], pg[:], [Bpg], [BqT])
                for r4 in range(4):
                    pg = ps_g[r4 % 2]
                    Bpg = Bps_g[r4 % 2]
                    for j in range(4):
                        hc = r4 * 4 + j
                        h.mm(pg[:, j, :], qT[:, hc, :], skT[:, hc, :], True, True, [BqT, Bwq], [Bpg])
                    h.cp("dve", eall[:, r4 * 4:r4 * 4 + 4, :], pg[:], [Bpg], [Beall])
                h.reduce("dve", sml[:, 0, :], eall[:], ALU.max, [Beall], [Bsml])
                h.tt("dve", eall[:], eall[:], bc3(sml[:, 0, :], 128), ALU.subtract, [Bsml], [Beall])
                h.act(eall[:], eall[:], AF.Exp, [], [Beall])
                for hc in range(16):
                    top16(eall[:, hc, :], sv[:, hc, :], [Beall], [Bsv])
                sv1 = sv[:, 0:16:2, :]
                sv2 = sv[:, 1:16:2, :]

                def mkcand(a1):
                    in0 = a1.unsqueeze(3).broadcast_to([128, 8, 16, 16])
                    in1 = sv2.unsqueeze(2).broadcast_to([128, 8, 16, 16])
                    h.tt("dve", cand[:], in0, in1, ALU.mult, [Bsv], [Bcand])
                    for hh in range(8):
                        top16(cand[:, hh, :, :], cv[:, hh, :], [Bcand], [Bcv_])
                mkcand(sv1)
                h.reduce("dve", sml[:, 2, 0:8], cv[:], ALU.add, [Bcv_], [Bsml])
                h.recip(sml[:, 3, 0:8], sml[:, 2, 0:8], [Bsml], [Bsml])
                h.tt("dve", e1n[w][:], eall[:, 0:16:2, :], bc3(sml[:, 3, 0:8], 128), ALU.mult, [Beall, Bsml], [Bst_[w]])
                h.tt("dve", sv1n[:], sv1, bc3(sml[:, 3, 0:8], 16), ALU.mult, [Bsv, Bsml], [Bsv])
                h.cp("pool", e2[w][:], eall[:, 1:16:2, :], [Beall], [Bst_[w]])
                mkcand(sv1n[:])
                h.cp("dve", thr[w][:], cv[:, :, 15], [Bcv_], [Bst_[w]])
            for eb in range(16):
                s = s0 if eb == 0 else snext
                if eb + 1 < 16:
                    snext = wload(eb + 1)
                U = ub[s]
                V = vb[s]
                for w in range(2):
                    for half in range(2):
                        for kc in range(8):
                            h.mm(ps_a[:], xnT[:, kc, w, :], U[:, kc, half * 512:(half + 1) * 512], kc == 0, kc == 7, [Bub[s], BxnT], [Bps_a])
                        h.act(ga[:, half * 512:(half + 1) * 512], ps_a[:], AF.Gelu, [Bps_a], [Bga])
                    for hh in range(8):
                        Eb = E[hh % 2]
                        Gb = Gh[hh % 2]
                        in0 = e1n[w][:, hh, eb * 8:(eb + 1) * 8].unsqueeze(2).broadcast_to([128, 8, 128])
                        in1 = e2[w][:, hh, :].unsqueeze(1).broadcast_to([128, 8, 128])
                        h.tt("dve", Eb[:], in0, in1, ALU.mult, [Bst_[w]], [BE[hh % 2]])
                        h.stt(Gb[:], Eb[:].rearrange("p a b -> p (a b)"), thr[w][:, hh:hh + 1], Eb[:].rearrange("p a b -> p (a b)"), ALU.is_ge, ALU.mult,
                              [BE[hh % 2], Bst_[w]], [BGh[hh % 2]])
                        for half in range(2):
                            h.mm(ps_g[half][:].rearrange("p a b -> p (a b)"), ident[:], Gb[:, half * 512:(half + 1) * 512], hh == 0, hh == 7,
                                 [BGh[hh % 2], CB], [Bps_g[half]])
                    for half in range(2):
                        h.cp("act", Gsb[:, half * 512:(half + 1) * 512], ps_g[half][:].rearrange("p a b -> p (a b)"), [Bps_g[half]], [BGsb])
                    h.tt("pool", Asb[:], Gsb[:], ga[:], ALU.mult, [BGsb, Bga], [BAsb])
                    for sub in range(8):
                        h.tr(ps_t[:, sub, :], Asb[:, sub * 128:(sub + 1) * 128], ident[:], [BAsb, CB], [Bps_t])
                    A = aT[w]
                    h.cp("act", A[:], ps_t[:], [Bps_t], [BaT[w]])
                    for sub in range(8):
                        for nb in range(2):
                            h.mm(ps_o[w][nb][:], A[:, sub, :], V[:, sub, nb * 512:(nb + 1) * 512], (eb == 0 and sub == 0), (eb == 15 and sub == 7),
                                 [BaT[w], Bvb[s]], [Bps_o[w][nb]])
            for w in range(2):
                c = ps_i * 2 + w
                for nb in range(2):
                    h.tt("dve", ho[w][:, nb * 512:(nb + 1) * 512], ht[w][:, nb * 512:(nb + 1) * 512], ps_o[w][nb][:], ALU.add, [Bps_o[w][nb], Bht[w]], [Bho[w]])
                p.dma("sp", f"d_ho{w}", dr[dst][c], ho[w][:], [Bho[w]], [dr["B_" + dst]])
        p.finalize([dr["B_" + dst]], ["d_ho0", "d_ho1"])
        p.build()


def phase_qkv(nc, p, h, NT, NT1, dr, consts, src):
    es = ExitStack()
    with es:
        UID[0] += 1
        uid = UID[0]

        def sb(name, shape, dt):
            return es.enter_context(nc.sbuf_tensor(f"{name}_u{uid}", list(shape), dt))

        def pst(name, shape, dt):
            return es.enter_context(nc.psum_tensor(f"{name}_u{uid}", list(shape), dt))
        p.barrier()
        CB = consts["B"]
        ident = consts["ident_bf"]
        wqkv = sb("wqkv", [128, 8, 1536], BF16)
        Bw = Buf("wqkv")
        for kc in range(8):
            p.dma("pool", "d_wqkv", wqkv[:, kc, :], dr["w_qkv"][kc * 128:(kc + 1) * 128, :], (), [Bw])
        gm = sb("gm1", [128, D], F32)
        p.dma("sp", "d_gm1", gm[:], dr["g_mix1"].partition_broadcast(128), (), [Bw])
        gain = sb("gain", [128, 10, 128], F32)
        for hh in range(8):
            p.dma("sp", "d_gm1", gain[:, hh, :], dr["g_q"].partition_broadcast(128), (), [Bw])
        for hh in range(8, 10):
            p.dma("sp", "d_gm1", gain[:, hh, :], dr["g_k"].partition_broadcast(128), (), [Bw])
        h.ts("dve", gain[:, 0:8, :], gain[:, 0:8, :], 128 ** -0.5, None, ALU.mult, None, [Bw], [Bw])
        ht = [sb(f"ht{i}", [128, D], F32) for i in range(2)]
        Bht = [Buf("ht") for _ in range(2)]
        cs_t = [sb(f"cs{i}", [128, 2, 128], F32) for i in range(2)]
        Bcs = [Buf("cs") for _ in range(2)]
        scr = {"junk": sb("junkq", [128, D], BF16), "ssq": sb("ssqq", [128, 1], F32), "rstd": sb("rstdq", [128, 1], F32), "B": Buf("nscrq")}
        xn = sb("xn", [128, D], BF16)
        Bxn = Buf("xn")
        xnT = sb("xnT", [128, 8, 128], BF16)
        BxnT = Buf("xnT")
        qkv = sb("qkv", [128, 12, 128], F32)
        Bqkv = Buf("qkv")
        sq = sb("sq", [128, 10, 128], F32)
        Bsq = Buf("sq")
        t2 = sb("t2", [128, 10, 128], F32)
        Bt2 = Buf("t2")
        sm = sb("smq", [128, 10], F32)
        Bsm = Buf("smq")
        qr = sb("qr", [128, 10, 128], BF16)
        Bqr = Buf("qr")
        vst = [sb(f"vst{i}", [128, 2, 128], BF16) for i in range(2)]
        Bvst = [Buf("vst") for _ in range(2)]
        qTs = [sb(f"qTs{i}", [128, 8, 128], BF16) for i in range(2)]
        BqTs = [Buf("qTs") for _ in range(2)]
        kTs = [sb(f"kTs{i}", [128, 2, 128], BF16) for i in range(2)]
        BkTs = [Buf("kTs") for _ in range(2)]
        ps_t = pst("ps_t", [128, 8, 128], BF16)
        Bps_t = Buf("ps_t")
        ps_k = pst("ps_k", [128, 8, 128], BF16)
        Bps_k = Buf("ps_k")
        ps_a = [pst(f"ps_a{i}", [128, 512], F32) for i in range(3)]
        Bps_a = [Buf("ps_a") for _ in range(3)]

        def loads(c):
            s = c % 2
            p.dma("sp", f"d_ht{s}", ht[s][:], dr[src][c], [dr["B_" + src]], [Bht[s]])
            p.dma("sp", f"d_cs{s}", cs_t[s][:], dr["rope"][c], (), [Bcs[s]])
        loads(0)
        for c in range(NT):
            s = c % 2
            if c + 1 < NT:
                loads(c + 1)
            rmsnorm_tile(h, ht[s][:], gm[:], xn[:], D, scr, [Bht[s], Bw], [Bxn], "q")
            for kc in range(8):
                h.tr(ps_t[:, kc, :], xn[:, kc * 128:(kc + 1) * 128], ident[:], [Bxn, CB], [Bps_t])
            h.cp("act", xnT[:], ps_t[:], [Bps_t], [BxnT])
            for nb in range(3):
                for kc in range(8):
                    h.mm(ps_a[nb][:], xnT[:, kc, :], wqkv[:, kc, nb * 512:(nb + 1) * 512], kc == 0, kc == 7, [BxnT, Bw], [Bps_a[nb]])
                h.cp("act" if nb == 1 else "dve", qkv[:, nb * 4:(nb + 1) * 4, :].rearrange("p a b -> p (a b)"), ps_a[nb][:], [Bps_a[nb]], [Bqkv])
            qk = qkv[:, 0:10, :]
            h.tt("dve", sq[:], qk, qk, ALU.mult, [Bqkv], [Bsq])
            h.reduce("dve", sm[:], sq[:], ALU.add, [Bsq], [Bsm])
            h.ts("dve", sm[:], sm[:], 1.0 / 128, EPS, ALU.mult, ALU.add, [Bsm], [Bsm])
            h.act(sm[:], sm[:], AF.Sqrt, [Bsm], [Bsm])
            h.recip(sm[:], sm[:], [Bsm], [Bsm])
            h.tt("dve", sq[:], qk, bc3(sm[:], 128), ALU.mult, [Bqkv, Bsm], [Bsq])
            h.tt("pool", sq[:], sq[:], gain[:], ALU.mult, [Bw], [Bsq])
            cosb = cs_t[s][:, 0, :].unsqueeze(1).broadcast_to([128, 10, 128])
            h.tt("dve", t2[:, :, 0:64], sq[:, :, 64:128], cs_t[s][:, 1, 0:64].unsqueeze(1).broadcast_to([128, 10, 64]), ALU.mult, [Bsq, Bcs[s]], [Bt2])
            h.tt("dve", t2[:, :, 64:128], sq[:, :, 0:64], cs_t[s][:, 1, 64:128].unsqueeze(1).broadcast_to([128, 10, 64]), ALU.mult, [Bsq, Bcs[s]], [Bt2])
            h.tt("pool", sq[:], sq[:], cosb, ALU.mult, [Bcs[s]], [Bsq])
            h.tt("dve", qr[:], sq[:], t2[:], ALU.add, [Bsq, Bt2], [Bqr])
            h.cp("pool", vst[s][:], qkv[:, 10:12, :], [Bqkv], [Bvst[s]])
            p.dma("sp", f"d_vst{s}", dr["v_s"][c], vst[s][:], [Bvst[s]], [dr["B_v_s"]])
            for g in range(2):
                h.tr(ps_k[:, g, :], qr[:, 8 + g, :], ident[:], [Bqr, CB], [Bps_k])
            h.cp("act", kTs[s][:], ps_k[:, 0:2, :], [Bps_k], [BkTs[s]])
            p.dma("sp", f"d_kTs{s}", dr["kT_s"][c], kTs[s][:], [BkTs[s]], [dr["B_kT_s"]])
            if c < NT1:
                for hh in range(8):
                    h.tr(ps_t[:, hh, :], qr[:, hh, :], ident[:], [Bqr, CB], [Bps_t])
                h.cp("act", qTs[s][:], ps_t[:], [Bps_t], [BqTs[s]])
                p.dma("sp", f"d_qTs{s}", dr["qT_s"][c], qTs[s][:], [BqTs[s]], [dr["B_qT_s"]])
        p.finalize([dr["B_v_s"]], ["d_vst0", "d_vst1"])
        p.finalize([dr["B_kT_s"]], ["d_kTs0", "d_kTs1"])
        p.finalize([dr["B_qT_s"]], ["d_qTs0", "d_qTs1"])
        p.build()


def phase_attn(nc, p, h, NT, NT1, dr, consts, src, dst):
    es = ExitStack()
    with es:
        UID[0] += 1
        uid = UID[0]

        def sb(name, shape, dt):
            return es.enter_context(nc.sbuf_tensor(f"{name}_u{uid}", list(shape), dt))

        def pst(name, shape, dt):
            return es.enter_context(nc.psum_tensor(f"{name}_u{uid}", list(shape), dt))
        p.barrier()
        CB = consts["B"]
        ident = consts["ident_bf"]
        KT = sb("KT", [128, NT, 2, 128], BF16)
        Vs = sb("Vs", [128, NT, 2, 129], BF16)
        BKV = Buf("KV")
        h.memset("pool", Vs[:, :, :, 128:129], 1.0, [BKV])
        CH = 13
        for c0 in range(0, NT, CH):
            c1 = min(NT, c0 + CH)
            p.dma("sp", "d_KT", KT[:, c0:c1, :, :], dr["kT_s"][c0:c1].rearrange("c p g k -> p c g k"), [dr["B_kT_s"]], [BKV])
            for g_ in range(2):
                p.dma("sp", "d_KT", Vs[:, c0:c1, g_, 0:128], dr["v_s"][c0:c1, :, g_, :].rearrange("c p k -> p c k"), [dr["B_v_s"]], [BKV])
        kb = sb("kb", [128, NT], F32)
        p.dma("sp", "d_KT", kb[:], dr["kbias"][:, :], (), [BKV])
        wo = sb("wo", [128, 8, D], BF16)
        for kc in range(8):
            p.dma("pool", "d_wo", wo[:, kc, :], dr["w_o"][kc * 128:(kc + 1) * 128, :], (), [BKV])
        zer = sb("zer", [128, 512], F32)
        h.memset("pool", zer[:], 0.0, [BKV])
        qT = [sb(f"qT{i}", [128, 8, 128], BF16) for i in range(2)]
        BqT = [Buf("qT") for _ in range(2)]
        ht = [sb(f"ht{i}", [128, D], F32) for i in range(2)]
        Bht = [Buf("ht") for _ in range(2)]
        ho = [sb(f"ho{i}", [128, D], F32) for i in range(2)]
        Bho = [Buf("ho") for _ in range(2)]
        PT = [sb(f"PT{i}", [128, 512], BF16) for i in range(3)]
        BPT = [Buf("PT") for _ in range(3)]
        rs = sb("rs", [128, 2], F32)
        Brs = Buf("rs")
        otok = sb("otok", [128, 8, 128], BF16)
        Botok = Buf("otok")
        oT = sb("oT", [128, 8, 128], BF16)
        BoT = Buf("oT")
        ps_S = [pst(f"ps_S{i}", [128, 512], F32) for i in range(2)]
        Bps_S = [Buf("ps_S") for _ in range(2)]
        ps_O = [[pst(f"ps_O{g}{i}", [128, 512], F32) for i in range(2)] for g in range(2)]
        Bps_O = [[Buf("ps_O") for i in range(2)] for g in range(2)]
        ps_t = pst("ps_t", [128, 8, 128], BF16)
        Bps_t = Buf("ps_t")
        ps_w = pst("ps_w", [128, 512], F32)
        Bps_w = Buf("ps_w")
        it = 0

        def loads(j):
            s = j % 2
            p.dma("sp", f"d_qT{s}", qT[s][:], dr["qT_s"][j], [dr["B_qT_s"]], [BqT[s]])
            p.dma("sp", f"d_hta{s}", ht[s][:], dr[src][j], [dr["B_" + src]], [Bht[s]])
        loads(0)
        for j in range(NT1):
            s = j % 2
            if j + 1 < NT1:
                loads(j + 1)
            for g in range(2):
                for i2 in range(2):
                    h.cp("act", ps_O[g][i2][:], zer[:], [BKV], [Bps_O[g][i2]])
                for kt in range(NT):
                    pS = ps_S[it % 2]
                    BpS = Bps_S[it % 2]
                    P_ = PT[it % 3]
                    BP_ = BPT[it % 3]
                    it += 1
                    h.mm(pS[:], KT[:, kt, g, :], qT[s][:, g * 4:(g + 1) * 4, :], True, True, [BKV, BqT[s]], [BpS])
                    h.act(P_[:], pS[:], AF.Exp, [BpS, BKV], [BP_], bias=kb[:, kt:kt + 1])
                    for hq in range(4):
                        O = ps_O[g][hq // 2]
                        h.mm_acc(O[:, (hq % 2) * 129:(hq % 2) * 129 + 129], P_[:, hq * 128:(hq + 1) * 128], Vs[:, kt, g, :], kt == NT - 1, [BP_, BKV], [Bps_O[g][hq // 2]])
                for i2 in range(2):
                    O = ps_O[g][i2]
                    Ov = O[:, 0:258].rearrange("p (a b) -> p a b", b=129)
                    h.recip(rs[:], Ov[:, :, 128], [Bps_O[g][i2]], [Brs])
                    h0 = g * 4 + i2 * 2
                    h.tt("dve", otok[:, h0:h0 + 2, :], Ov[:, :, 0:128], bc3(rs[:], 128), ALU.mult, [Bps_O[g][i2], Brs], [Botok])
            for hh in range(8):
                h.tr(ps_t[:, hh, :], otok[:, hh, :], ident[:], [Botok, CB], [Bps_t])
            h.cp("act", oT[:], ps_t[:], [Bps_t], [BoT])
            for nb in range(2):
                for kc in range(8):
                    h.mm(ps_w[:], oT[:, kc, :], wo[:, kc, nb * 512:(nb + 1) * 512], kc == 0, kc == 7, [BoT, BKV], [Bps_w])
                h.tt("dve", ho[s][:, nb * 512:(nb + 1) * 512], ht[s][:, nb * 512:(nb + 1) * 512], ps_w[:], ALU.add, [Bps_w, Bht[s]], [Bho[s]])
            p.dma("sp", f"d_hoa{s}", dr[dst][j], ho[s][:], [Bho[s]], [dr["B_" + dst]])
        p.finalize([dr["B_" + dst]], ["d_hoa0", "d_hoa1"])
        p.build()


BIG = 30000.0
def consts_setup(nc, p, h, es, dr):
    c = {}
    c["B"] = Buf("consts")
    cm = es.enter_context(nc.sbuf_tensor("cmats_sb", [128, 6, 128], F32))
    p.dma("sp", "d_const", cm[:], dr["cmats"][:, :, :], (), [c["B"]])
    c["ident_f"] = cm[:, 0, :]
    c["tri"] = [cm[:, 1, :], cm[:, 2, :]]
    c["mneg"] = [cm[:, 3, :], cm[:, 4, :]]
    c["ones"] = cm[:, 5, :]
    c["ident_bf"] = es.enter_context(nc.sbuf_tensor("ident_bf", [128, 128], BF16))
    h.cp("dve", c["ident_bf"][:], c["ident_f"], [c["B"]], [c["B"]])
    return c

def cmats_np():
    k = np.arange(128)[:, None]; l = np.arange(128)[None, :]
    m = np.zeros((128, 6, 128), np.float32)
    m[:, 0] = np.eye(128)
    m[:, 1] = (k <= l); m[:, 2] = (k >= l)
    m[:, 3] = np.where(l < k, -BIG, 0); m[:, 4] = np.where(l > k, -BIG, 0)
    m[:, 5] = 1
    return m


def rope_tables(pos_kind, pos_idx):
    T = pos_kind.shape[0]
    row = np.where(pos_kind == 2, pos_idx // 64, -1).astype(np.float32)
    col = np.where(pos_kind == 2, pos_idx % 64, pos_idx).astype(np.float32)
    row = np.where(pos_kind == 0, 0, row).astype(np.float32)
    col = np.where(pos_kind == 0, 0, col).astype(np.float32)
    inv = (np.float32(10000.0) ** (-np.arange(32, dtype=np.float32) / np.float32(32))).astype(np.float32)
    ang = np.concatenate([row[:, None] * inv, col[:, None] * inv], axis=-1).astype(np.float32)
    ang = np.concatenate([ang, ang], axis=-1)
    cos = np.cos(ang).astype(np.float32)
    sin = np.sin(ang).astype(np.float32)
    sin[:, :64] *= -1
    return np.stack([cos, sin], axis=1).astype(np.float32)


NT_FULL = 130
NT1_FULL = 66
S_PROMPT = 8192
S_SAMPLE = 16384


def build_program(NT, NT1):
    nc = bass.Bass("TRN2", target_bir_lowering=False)
    dr = {}

    def din(name, shape, dt=F32):
        dr[name] = nc.dram_tensor(name, shape, dt, kind="ExternalInput").ap()
        dr["B_" + name] = Buf(name)

    def dsc(name, shape, dt=F32, kind="Internal"):
        dr[name] = nc.dram_tensor(name, shape, dt, kind=kind).ap()
        dr["B_" + name] = Buf(name)
    din("x", [NT, 128, D]); din("w_in", [D, 5184]); din("conv_w", [128, 24, 5]); din("conv_b", [128, 24]); din("g_mix0", [D])
    din("cmats", [128, 6, 128]); din("dt_bias", [64]); din("a_log", [64]); din("d_skip", [32]); din("valid", [128, NT])
    din("w_out", [DIN, D]); din("gate_g", [DIN])
    din("peer_wq", [2, D, 2048]); din("peer_skT", [2, 128, 16, 128]); din("g_ffn", [2, D])
    din("uT32", [2, 16, 128, 8192]); din("v32", [2, 16, 128, 8192])
    din("w_qkv", [D, 1536]); din("g_mix1", [D]); din("g_q", [128]); din("g_k", [128])
    din("rope", [NT, 128, 2, 128]); din("kbias", [128, NT]); din("w_o", [D, D])
    dsc("xbca", [NT, 128, 24 * 128], BF16); dsc("zdt", [NT, 128, NZD]); dsc("ybwd", [NT, 128, DIN]); dsc("h1", [NT, 128, D]); dsc("h2", [NT, 128, D])
    dsc("uT16", [2, 16, 128, 8192], BF16); dsc("v16", [2, 16, 128, 8192], BF16)
    dsc("kT_s", [NT, 128, 2, 128], BF16); dsc("v_s", [NT, 128, 2, 128], BF16); dsc("qT_s", [NT1, 128, 8, 128], BF16)
    dsc("h3", [NT1, 128, D])
    dsc("y", [NT1, 128, D], F32, kind="ExternalOutput")
    with ExitStack() as es:
        p = Prog(nc, es)
        h = H(p)
        consts = consts_setup(nc, p, h, es, dr)
        p.build()
        phase0(nc, p, h, dr, 0)
        phase0(nc, p, h, dr, 1)
        phase1(nc, p, h, NT, dr, consts)
        p.build()
        phase2(nc, p, h, NT, dr, consts, 1, False)
        phase2(nc, p, h, NT, dr, consts, 0, True)
        phase_peer(nc, p, h, NT, dr, consts, 0, "h1", "h2")
        phase_qkv(nc, p, h, NT, NT1, dr, consts, "h2")
        phase_attn(nc, p, h, NT, NT1, dr, consts, "h2", "h3")
        phase_peer(nc, p, h, NT1, dr, consts, 1, "h3", "y")
        p.final_wait("sp", [dr["B_y"]])
        p.build()
    return nc, p.nops


def pack_a(xseq, meta, T):
    S = xseq.shape[0]
    x = np.zeros((T, D), np.float32)
    kind = np.zeros(T, np.int64)
    idx = np.zeros(T, np.int64)
    x[112:128] = meta
    kind[112:128] = 1
    idx[112:128] = np.arange(16)
    n = min(S, T - 128)
    x[128:128 + n] = xseq[:n]
    kind[128:128 + n] = 2
    idx[128:128 + n] = np.arange(n)
    return x, kind, idx


def kernel(x_prompt, x_sample, meta_tokens, norm_mix_g, norm_ffn_g, ssd_w_in, ssd_conv_w, ssd_conv_b,
           ssd_dt_bias, ssd_a_log, ssd_d_skip, ssd_gate_norm_g, ssd_w_out, attn_w_qkv, attn_q_norm_g,
           attn_k_norm_g, attn_w_o, peer_w_q, peer_sub_keys, peer_u, peer_v, _NT=None, _NT1=None):
    f = lambda a: np.ascontiguousarray(np.asarray(a, dtype=np.float32))
    NT = _NT or NT_FULL
    NT1 = _NT1 or NT1_FULL
    T = NT * 128
    x_prompt, x_sample, meta = f(x_prompt), f(x_sample), f(meta_tokens)
    w_in = f(ssd_w_in)[0]
    conv_w = f(ssd_conv_w)[0]
    conv_b = f(ssd_conv_b)[0]
    dt_bias = f(ssd_dt_bias)[0]
    a_log = f(ssd_a_log)[0]
    U = f(peer_u)
    V = f(peer_v)
    sk = f(peer_sub_keys)
    common = {
        "conv_b": np.ascontiguousarray(conv_b.reshape(24, 128).T),
        "g_mix0": f(norm_mix_g)[0], "g_mix1": f(norm_mix_g)[1], "cmats": cmats_np(),
        "d_skip": f(ssd_d_skip)[0], "w_out": f(ssd_w_out)[0], "gate_g": f(ssd_gate_norm_g)[0],
        "peer_wq": f(peer_w_q), "g_ffn": f(norm_ffn_g),
        "peer_skT": np.ascontiguousarray(sk.reshape(2, 16, 128, 128).transpose(0, 3, 1, 2)),
        "uT32": np.ascontiguousarray(U.reshape(2, 16, 1024, 8, 128).transpose(0, 1, 4, 3, 2)).reshape(2, 16, 128, 8192),
        "v32": np.ascontiguousarray(V.reshape(2, 16, 8, 128, 1024).transpose(0, 1, 3, 2, 4)).reshape(2, 16, 128, 8192),
        "w_qkv": f(attn_w_qkv)[0], "g_q": f(attn_q_norm_g)[0], "g_k": f(attn_k_norm_g)[0], "w_o": f(attn_w_o)[0],
    }
    par_a = {"w_in": w_in, "conv_w": np.ascontiguousarray(conv_w.T.reshape(24, 128, 5).transpose(1, 0, 2)),
             "dt_bias": dt_bias.reshape(64).copy(), "a_log": a_log.reshape(64).copy()}
    w_in_b = w_in.copy()
    w_in_b[:, 5120:5152] = w_in[:, 5152:5184]
    w_in_b[:, 5152:5184] = w_in[:, 5120:5152]
    par_b = {"w_in": w_in_b, "conv_w": np.ascontiguousarray(conv_w[::-1].T.reshape(24, 128, 5).transpose(1, 0, 2)),
             "dt_bias": np.ascontiguousarray(dt_bias[::-1]).reshape(64), "a_log": np.ascontiguousarray(a_log[::-1]).reshape(64)}
    plan = [("p", 0, "a"), ("p", 1, "a"), ("p", 2, "a"), ("p", 3, "a"), ("s", 0, "a"), ("s", 0, "b"), ("s", 1, "a"), ("s", 1, "b")]
    in_maps = []
    for (grp, bi, typ) in plan:
        xs = x_prompt[bi] if grp == "p" else x_sample[bi]
        x, kind, idx = pack_a(xs, meta, T)
        if typ == "b":
            x, kind, idx = x[::-1], kind[::-1], idx[::-1]
        m = dict(common)
        m.update(par_a if typ == "a" else par_b)
        m["x"] = np.ascontiguousarray(x).reshape(NT, 128, D)
        m["valid"] = np.ascontiguousarray((kind > 0).astype(np.float32).reshape(NT, 128).T)
        m["kbias"] = np.ascontiguousarray(np.where(kind > 0, 0.0, -BIG).astype(np.float32).reshape(NT, 128).T)
        m["rope"] = np.ascontiguousarray(rope_tables(np.ascontiguousarray(kind), np.ascontiguousarray(idx))).reshape(NT, 128, 2, 128)
        in_maps.append(m)
    nc, nops = build_program(NT, NT1)
    res = run_bass_kernel_spmd(nc, in_maps, core_ids=list(range(8)))
    ys = [np.asarray(r["y"]).reshape(NT1 * 128, D) for r in res.results]
    y_prompt = np.zeros(x_prompt.shape, np.float32)
    y_sample = np.zeros(x_sample.shape, np.float32)
    for b in range(4):
        n = min(x_prompt.shape[1], NT1 * 128 - 128)
        y_prompt[b, :n] = ys[b][128:128 + n]
    for s_ in range(2):
        ya, yb = ys[4 + 2 * s_], ys[5 + 2 * s_]
        na = min(NT1 * 128 - 128, x_sample.shape[1])
        y_sample[s_, :na] = ya[128:128 + na]
        t = np.arange(na, x_sample.shape[1])
        r = T - 1 - (128 + t)
        ok = (r >= 0) & (r < NT1 * 128)
        y_sample[s_, t[ok]] = yb[r[ok]]
    return (y_prompt, y_sample)
```
